# Optimizing a Trainium2 kernel written in Bass

```python
import math
import jax, jax.numpy as jnp
from jax import lax
import numpy as np

D_MODEL = 4096
BATCH = 4
SEQ = 4096
DEPTH = 2

SSD_EXPAND = 2
SSD_D_INNER = SSD_EXPAND * D_MODEL
SSD_HEAD_DIM = 64
SSD_HEADS = SSD_D_INNER // SSD_HEAD_DIM
SSD_GROUPS = 8
SSD_STATE = 128
SSD_CONV = 4
SSD_CHUNK = 128
SSD_CONV_CH = SSD_D_INNER + 2 * SSD_GROUPS * SSD_STATE
ATT_HEADS = 32
ATT_HEAD_DIM = 128
ATT_KV_GROUPS = 4
ATT_WIDTH = ATT_HEADS * ATT_HEAD_DIM
ATT_KV_WIDTH = ATT_KV_GROUPS * ATT_HEAD_DIM
IDX_HEADS = 32
IDX_DIM = 128
TOPK_MAX = 256
Q_BLOCK = 128
D_FF = 2 * D_MODEL
FFN_CONV = 3
N_MOD = 6
EPS = 1e-6

IN_SPLIT_SIZES = (SSD_D_INNER, SSD_CONV_CH, SSD_HEADS, ATT_WIDTH, ATT_KV_WIDTH, ATT_KV_WIDTH,
                  IDX_HEADS * IDX_DIM, IDX_DIM, IDX_HEADS, D_MODEL, D_MODEL)
IN_SPLIT_POINTS = tuple(int(v) for v in np.cumsum(IN_SPLIT_SIZES)[:-1])
N_IN = int(sum(IN_SPLIT_SIZES))

kernel_name = 'hybrid_ssd_dsa_convffn_block'


def rms_norm(x, g):
    xf = x.astype(jnp.float32)
    y = xf * lax.rsqrt(jnp.mean(xf * xf, axis=-1, keepdims=True) + EPS)
    return (y * g.astype(jnp.float32)).astype(x.dtype)


def causal_dwconv(x, w, b):
    k = w.shape[0]
    y = lax.conv_general_dilated(
        x, w[:, None, :].astype(x.dtype), window_strides=(1,), padding=[(k - 1, 0)],
        dimension_numbers=('NWC', 'WIO', 'NWC'), feature_group_count=x.shape[-1])
    return y + b.astype(x.dtype)


def ssd_chunked(x, dt, a, bm, cm):
    bsz, s, h, p = x.shape
    g, n = bm.shape[2], bm.shape[3]
    hg = h // g
    nc = s // SSD_CHUNK
    q = SSD_CHUNK
    xdt = jnp.moveaxis((x * dt[..., None]).reshape(bsz, nc, q, g, hg, p), 1, 0)
    da = jnp.moveaxis((dt * a).reshape(bsz, nc, q, g, hg), 1, 0)
    bc = jnp.moveaxis(bm.reshape(bsz, nc, q, g, n), 1, 0)
    cc = jnp.moveaxis(cm.reshape(bsz, nc, q, g, n), 1, 0)
    causal = jnp.tril(jnp.ones((q, q), dtype=bool))[None, :, :, None, None]

    def step(state, inp):
        xq, daq, bq, cq = inp
        acum = jnp.cumsum(daq, axis=1)
        seg = acum[:, :, None] - acum[:, None, :]
        decay = jnp.exp(jnp.where(causal, seg, -jnp.inf))
        cb = jnp.einsum('bign,bjgn->bijg', cq, bq)
        y_diag = jnp.einsum('bijg,bijgh,bjghp->bighp', cb, decay, xq)
        y_off = jnp.einsum('bign,bghpn,bigh->bighp', cq, state, jnp.exp(acum))
        last = acum[:, -1]
        w_state = jnp.exp(last[:, None] - acum)
        new_state = state * jnp.exp(last)[..., None, None] + jnp.einsum(
            'bjgn,bjgh,bjghp->bghpn', bq, w_state, xq)
        return new_state, y_diag + y_off

    state0 = jnp.zeros((bsz, g, hg, p, n), jnp.float32)
    _, y = lax.scan(step, state0, (xdt, da, bc, cc))
    return jnp.moveaxis(y, 0, 1).reshape(bsz, s, h, p)


def ssd_branch(z, xbc, dt_raw, conv_w, conv_b, dt_bias, a_log, d_skip, norm_g):
    bsz, s, _ = z.shape
    f32 = jnp.float32
    xbc = jax.nn.silu(causal_dwconv(xbc, conv_w, conv_b)).astype(f32)
    gn = SSD_GROUPS * SSD_STATE
    xs, bm, cm = jnp.split(xbc, [SSD_D_INNER, SSD_D_INNER + gn], axis=-1)
    dt = jax.nn.softplus(dt_raw.astype(f32) + dt_bias.astype(f32))
    a = -jnp.exp(a_log.astype(f32))
    xh = xs.reshape(bsz, s, SSD_HEADS, SSD_HEAD_DIM)
    y = ssd_chunked(xh, dt, a,
                    bm.reshape(bsz, s, SSD_GROUPS, SSD_STATE),
                    cm.reshape(bsz, s, SSD_GROUPS, SSD_STATE))
    y = y + d_skip.astype(f32)[:, None] * xh
    yg = (y.reshape(bsz, s, SSD_D_INNER) * jax.nn.silu(z.astype(f32))).reshape(
        bsz, s, SSD_GROUPS, SSD_D_INNER // SSD_GROUPS)
    yg = yg * lax.rsqrt(jnp.mean(yg * yg, axis=-1, keepdims=True) + EPS)
    y = yg.reshape(bsz, s, SSD_D_INNER) * norm_g.astype(f32)
    return y.astype(z.dtype)


def dsa_branch(q, k, v, qi, ki, wi, q_norm_g, k_norm_g, ki_norm_g):
    bsz, s, _ = q.shape
    f32 = jnp.float32
    hpg = ATT_HEADS // ATT_KV_GROUPS
    q = rms_norm(q.reshape(bsz, s, ATT_HEADS, ATT_HEAD_DIM), q_norm_g)
    k = rms_norm(k.reshape(bsz, s, ATT_KV_GROUPS, ATT_HEAD_DIM), k_norm_g)
    v = v.reshape(bsz, s, ATT_KV_GROUPS, ATT_HEAD_DIM)
    qi = qi.reshape(bsz, s, IDX_HEADS, IDX_DIM)
    ki = rms_norm(ki, ki_norm_g)
    wi = wi.astype(f32) * (IDX_HEADS ** -0.5)
    k_sel = min(TOPK_MAX, s // 4)
    nb = s // Q_BLOCK
    key_pos = jnp.arange(s)
    bidx = jnp.arange(bsz)[:, None]

    def to_blocks(t):
        return t.reshape((bsz, nb, Q_BLOCK) + t.shape[2:]).swapaxes(0, 1)

    def block(inp):
        qb, qib, wib, t0 = inp
        tpos = t0 + jnp.arange(Q_BLOCK)
        logits = jnp.einsum('bthd,bsd->bths', qib, ki,
                            preferred_element_type=f32) * (IDX_DIM ** -0.5)
        score = jnp.einsum('bth,bths->bts', wib, jax.nn.relu(logits))
        causal = key_pos[None, :] <= tpos[:, None]
        score = jnp.where(causal[None], score, -jnp.inf)
        _, idx = lax.top_k(score, k_sel)
        valid = idx <= tpos[None, :, None]
        flat = idx.reshape(bsz, Q_BLOCK * k_sel)
        kg = k[bidx, flat].reshape(bsz, Q_BLOCK, k_sel, ATT_KV_GROUPS, ATT_HEAD_DIM)
        vg = v[bidx, flat].reshape(bsz, Q_BLOCK, k_sel, ATT_KV_GROUPS, ATT_HEAD_DIM)
        qg = qb.reshape(bsz, Q_BLOCK, ATT_KV_GROUPS, hpg, ATT_HEAD_DIM)
        att = jnp.einsum('btghd,btkgd->btghk', qg, kg,
                         preferred_element_type=f32) * (ATT_HEAD_DIM ** -0.5)
        att = jnp.where(valid[:, :, None, None, :], att, -jnp.inf)
        prob = jax.nn.softmax(att, axis=-1).astype(vg.dtype)
        o = jnp.einsum('btghk,btkgd->btghd', prob, vg)
        return o.reshape(bsz, Q_BLOCK, ATT_WIDTH)

    starts = jnp.arange(nb) * Q_BLOCK
    out = lax.map(block, (to_blocks(q), to_blocks(qi), to_blocks(wi), starts))
    return out.swapaxes(0, 1).reshape(bsz, s, ATT_WIDTH)


def setup_inputs(seed: int = 0) -> dict:
    key = jax.random.key(seed)
    ks = jax.random.split(key, 24)
    f32 = jnp.float32

    def nrm(k, shape, scale):
        return jax.random.normal(k, shape, f32) * scale

    dt0 = jnp.exp(jax.random.uniform(ks[9], (DEPTH, SSD_HEADS), f32,
                                     math.log(1e-3), math.log(1e-1)))
    return {
        'x': nrm(ks[0], (BATCH, SEQ, D_MODEL), 1.0),
        'c': nrm(ks[1], (BATCH, D_MODEL), 1.0),
        'w_ada': nrm(ks[2], (D_MODEL, N_MOD * D_MODEL), 0.5 * D_MODEL ** -0.5),
        'b_ada': nrm(ks[3], (N_MOD * D_MODEL,), 0.02),
        'ada_table': nrm(ks[4], (DEPTH, N_MOD, D_MODEL), 0.02),
        'norm1_g': 1.0 + nrm(ks[5], (DEPTH, D_MODEL), 0.02),
        'w_in': nrm(ks[6], (DEPTH, D_MODEL, N_IN), D_MODEL ** -0.5),
        'ssd_conv_w': nrm(ks[7], (DEPTH, SSD_CONV, SSD_CONV_CH), SSD_CONV ** -0.5),
        'ssd_conv_b': nrm(ks[8], (DEPTH, SSD_CONV_CH), 0.02),
        'ssd_dt_bias': dt0 + jnp.log(-jnp.expm1(-dt0)),
        'ssd_a_log': jnp.log(jax.random.uniform(ks[10], (DEPTH, SSD_HEADS), f32, 1.0, 16.0)),
        'ssd_d': 1.0 + nrm(ks[11], (DEPTH, SSD_HEADS), 0.02),
        'ssd_norm_g': 1.0 + nrm(ks[12], (DEPTH, SSD_D_INNER), 0.02),
        'w_ssd_out': nrm(ks[13], (DEPTH, SSD_D_INNER, D_MODEL), SSD_D_INNER ** -0.5),
        'q_norm_g': 1.0 + nrm(ks[14], (DEPTH, ATT_HEAD_DIM), 0.02),
        'k_norm_g': 1.0 + nrm(ks[15], (DEPTH, ATT_HEAD_DIM), 0.02),
        'idx_k_norm_g': 1.0 + nrm(ks[16], (DEPTH, IDX_DIM), 0.02),
        'w_att_out': nrm(ks[17], (DEPTH, ATT_WIDTH, D_MODEL), ATT_WIDTH ** -0.5),
        'w_o': nrm(ks[18], (DEPTH, D_MODEL, D_MODEL), D_MODEL ** -0.5),
        'norm2_g': 1.0 + nrm(ks[19], (DEPTH, D_MODEL), 0.02),
        'w_up': nrm(ks[20], (DEPTH, D_MODEL, 2 * D_FF), D_MODEL ** -0.5),
        'ffn_conv_w': nrm(ks[21], (DEPTH, FFN_CONV, 2 * D_FF), FFN_CONV ** -0.5),
        'ffn_conv_b': nrm(ks[22], (DEPTH, 2 * D_FF), 0.02),
        'w_down': nrm(ks[23], (DEPTH, D_FF, D_MODEL), D_FF ** -0.5),
    }


def reference(x, c, w_ada, b_ada, ada_table, norm1_g, w_in, ssd_conv_w, ssd_conv_b,
              ssd_dt_bias, ssd_a_log, ssd_d, ssd_norm_g, w_ssd_out, q_norm_g, k_norm_g,
              idx_k_norm_g, w_att_out, w_o, norm2_g, w_up, ffn_conv_w, ffn_conv_b, w_down):
    bsz = x.shape[0]
    mod_shared = (jax.nn.silu(c) @ w_ada + b_ada).reshape(bsz, N_MOD, D_MODEL)
    for l in range(DEPTH):
        mod = mod_shared + ada_table[l][None]
        shift1, scale1, gate1, shift2, scale2, gate2 = [mod[:, i, None, :] for i in range(N_MOD)]

        h = rms_norm(x, norm1_g[l]) * (1.0 + scale1) + shift1
        proj = h @ w_in[l]
        z, xbc, dt_raw, q, k, v, qi, ki, wi, g_ssd, g_att = jnp.split(
            proj, IN_SPLIT_POINTS, axis=-1)
        y_ssd = ssd_branch(z, xbc, dt_raw, ssd_conv_w[l], ssd_conv_b[l], ssd_dt_bias[l],
                           ssd_a_log[l], ssd_d[l], ssd_norm_g[l]) @ w_ssd_out[l]
        y_att = dsa_branch(q, k, v, qi, ki, wi, q_norm_g[l], k_norm_g[l],
                           idx_k_norm_g[l]) @ w_att_out[l]
        merged = jax.nn.sigmoid(g_ssd) * y_ssd + jax.nn.sigmoid(g_att) * y_att
        x = x + gate1 * (merged @ w_o[l])

        h = rms_norm(x, norm2_g[l]) * (1.0 + scale2) + shift2
        up = causal_dwconv(h @ w_up[l], ffn_conv_w[l], ffn_conv_b[l])
        a, b = jnp.split(up, 2, axis=-1)
        x = x + gate2 * ((jax.nn.silu(a) * b) @ w_down[l])
    return x
```

```python
import os
from contextlib import ExitStack
import numpy as np
import concourse.bass as bass
import concourse.mybir as mybir
from concourse.bass_utils import run_bass_kernel_spmd

F32 = mybir.dt.float32
BF16 = mybir.dt.bfloat16
AF = mybir.ActivationFunctionType
ALU = mybir.AluOpType
EPS = 1e-6


class Cfg:
    def __init__(self, D=4096, S=4096, depth=2):
        self.D, self.S, self.depth = D, S, depth
        self.KC = D // 128
        self.DI = 2 * D
        self.H = self.DI // 64
        self.G = 8
        self.HG = self.H // 8
        self.GW = self.HG * 64
        self.CONVCH = self.DI + 2 * 8 * 128
        self.ATT = 4096
        self.KV = 512
        self.IDX = 4096
        self.DFF = 2 * D
        sizes = [self.DI, self.CONVCH, self.H, self.ATT, self.KV, self.KV, self.IDX, 128, 32, D, D]
        offs = np.concatenate([[0], np.cumsum(sizes)]).astype(int)
        (self.o_z, self.o_xbc, self.o_dt, self.o_q, self.o_k, self.o_v, self.o_qi, self.o_ki,
         self.o_wi, self.o_gs, self.o_ga, self.NIN) = [int(v) for v in offs]
        self.KSEL = min(256, S // 4)


class Prog:
    CE = ("pe", "act", "dve", "pool")
    NDS = 12

    def __init__(self, nc):
        self.nc = nc
        self.ops = []
        self.lastw = {}
        self.readers = {}
        self.pending_barrier = {}
        self.last_on = {}
        self.dmas_since = []

    def op(self, eng, fn, r=(), w=()):
        idx = len(self.ops)
        deps = set()
        for k in r:
            if k in self.lastw:
                deps.add(self.lastw[k])
        for k in w:
            if k in self.lastw:
                deps.add(self.lastw[k])
            deps.update(self.readers.get(k, ()))
        for k in w:
            self.lastw[k] = idx
            self.readers[k] = []
        for k in r:
            self.readers.setdefault(k, []).append(idx)
        if eng in self.pending_barrier:
            deps.update(self.pending_barrier.pop(eng))
        deps.discard(idx)
        self.ops.append(dict(eng=eng, fn=fn, deps=deps))
        self.last_on[eng] = idx
        if eng == "sp":
            self.dmas_since.append(idx)
        return idx

    def dma(self, out, in_, r=(), w=(), **kw):
        return self.op("sp", lambda e: e.dma_start(out=out, in_=in_, **kw), r, w)

    def barrier(self):
        b = set(self.last_on.values()) | set(self.dmas_since)
        self.dmas_since = []
        for e in self.CE + ("sp",):
            self.pending_barrier.setdefault(e, set()).update(b)
        self.lastw = {}
        self.readers = {}

    def finalize(self, es, block):
        nc = self.nc
        ops = self.ops
        needs = [False] * len(ops)
        for o in ops:
            for d in o["deps"]:
                if not (ops[d]["eng"] == "pe" and o["eng"] == "pe"):
                    needs[d] = True
        sems = {e: es.enter_context(nc.semaphore("s_" + e)) for e in self.CE}
        dsems = [es.enter_context(nc.semaphore("s_dma%d" % i)) for i in range(self.NDS)]
        cnt = {e: 0 for e in self.CE}
        dcount = [0] * self.NDS
        ndma = 0
        for i, o in enumerate(ops):
            if o["eng"] == "sp":
                s = ndma % self.NDS
                ndma += 1
                o["prev"] = (s, dcount[s])
                dcount[s] += 16
                o["done"] = (s, dcount[s])
            elif needs[i]:
                cnt[o["eng"]] += 1
                o["tick"] = cnt[o["eng"]]
        semobj = {}
        for e in self.CE:
            semobj[("c", e)] = sems[e]
        for i in range(self.NDS):
            semobj[("d", i)] = dsems[i]
        by_eng = {e: [] for e in self.CE + ("sp",)}
        for i, o in enumerate(ops):
            by_eng[o["eng"]].append(i)

        def emit(engname, eng):
            seen = {}
            for i in by_eng[engname]:
                o = ops[i]
                want = {}
                for d in o["deps"]:
                    do = ops[d]
                    if do["eng"] == "sp":
                        key, val = ("d", do["done"][0]), do["done"][1]
                    else:
                        if do["eng"] == "pe" and engname == "pe":
                            continue
                        key, val = ("c", do["eng"]), do["tick"]
                    if val > want.get(key, 0):
                        want[key] = val
                if engname == "sp":
                    s, v = o["prev"]
                    if v > 0 and v > want.get(("d", s), 0):
                        want[("d", s)] = v
                for key, val in want.items():
                    if val > seen.get(key, 0):
                        eng.wait_ge(semobj[key], val)
                        seen[key] = val
                ins = o["fn"](eng)
                if engname == "sp":
                    ins.then_inc(dsems[o["done"][0]], 16)
                elif "tick" in o:
                    ins.then_inc(sems[engname], 1)
            if engname == "sp":
                for s in range(self.NDS):
                    if dcount[s] > seen.get(("d", s), 0):
                        eng.wait_ge(dsems[s], dcount[s])

        @block.sync
        def _(e):
            emit("sp", e)

        @block.tensor
        def _(e):
            emit("pe", e)

        @block.scalar
        def _(e):
            emit("act", e)

        @block.vector
        def _(e):
            emit("dve", e)

        @block.gpsimd
        def _(e):
            emit("pool", e)


def build_nc(cfg, debug=False):
    D, S, KC, DI, H, G, HG, GW = cfg.D, cfg.S, cfg.KC, cfg.DI, cfg.H, cfg.G, cfg.HG, cfg.GW
    CONVCH, ATT, KV, IDX, DFF, NIN = cfg.CONVCH, cfg.ATT, cfg.KV, cfg.IDX, cfg.DFF, cfg.NIN
    L = cfg.depth
    NT = S // 512
    NB = S // 128
    nc = bass.Bass("TRN2", target_bir_lowering=False)

    def din(name, shape):
        return nc.dram_tensor(name, list(shape), F32, kind="ExternalInput").ap()

    x_in = din("x", [S, D])
    c_in = din("c", [D])
    w_ada = din("w_ada", [D, 6 * D])
    b_ada = din("b_ada", [6 * D])
    ada_table = din("ada_table", [L, 6, D])
    norm1_g = din("norm1_g", [L, D])
    w_in = din("w_in", [L, D, NIN])
    ssd_conv_w = din("ssd_conv_w", [L, 4, CONVCH])
    ssd_conv_b = din("ssd_conv_b", [L, CONVCH])
    ssd_dt_bias = din("ssd_dt_bias", [L, H])
    ssd_a_log = din("ssd_a_log", [L, H])
    ssd_d = din("ssd_d", [L, H])
    ssd_norm_g = din("ssd_norm_g", [L, DI])
    w_ssd_out = din("w_ssd_out", [L, DI, D])
    q_norm_g = din("q_norm_g", [L, 128])
    k_norm_g = din("k_norm_g", [L, 128])
    idx_k_norm_g = din("idx_k_norm_g", [L, 128])
    w_att_out = din("w_att_out", [L, ATT, D])
    w_o = din("w_o", [L, D, D])
    norm2_g = din("norm2_g", [L, D])
    w_up = din("w_up", [L, D, 2 * DFF])
    ffn_conv_w = din("ffn_conv_w", [L, 3, 2 * DFF])
    ffn_conv_b = din("ffn_conv_b", [L, 2 * DFF])
    w_down = din("w_down", [L, DFF, D])
    consts = din("consts", [128, 5 * 128])
    out = nc.dram_tensor("out", [S, D], F32, kind="ExternalOutput").ap()

    dbg_kind = "ExternalOutput" if debug else "Internal"

    def scratch(name, shape, dt=F32):
        return nc.dram_tensor(name, list(shape), dt, kind=dbg_kind).ap()

    xT = scratch("xT", [D, S])
    hT = scratch("hT", [D, S], BF16)
    PCH = 12288 if NIN > 20000 else NIN
    projC = [scratch("projT%d" % i, [min(PCH, NIN - i * PCH), S]) for i in range((NIN + PCH - 1) // PCH)]

    class _Rows:
        def __init__(self, chunks, ch):
            self.chunks, self.ch = chunks, ch

        def __getitem__(self, idx):
            rs, cs = idx
            r0 = rs.start or 0
            r1 = rs.stop
            ci = r0 // self.ch
            assert (r1 - 1) // self.ch == ci, ("row range straddles scratch chunk", r0, r1)
            return self.chunks[ci][r0 - ci * self.ch:r1 - ci * self.ch, cs]

    projT = _Rows(projC, PCH)
    xbcT = scratch("xbcT", [CONVCH, S], BF16)
    dtT = scratch("dtT", [H, S])
    acumT = scratch("acumT", [H, S])
    yT = scratch("yT", [DI, S])
    ynT = scratch("ynT", [DI, S], BF16)
    yssdT = scratch("yssdT", [D, S])
    qnT = scratch("qnT", [ATT, S], BF16)
    knT = scratch("knT", [KV, S], BF16)
    kinT = scratch("kinT", [128, S], BF16)
    vtok = scratch("vtok", [S, KV], BF16)
    qiT = scratch("qiT", [IDX, S], BF16)
    witok = scratch("witok", [S, 32])
    maskT = scratch("maskT", [S, S], BF16)
    oT = scratch("oT", [ATT, S], BF16)
    yattT = scratch("yattT", [D, S])
    mT = scratch("mT", [D, S], BF16)
    pT = scratch("pT", [D, S])
    upC = [scratch("upT%d" % i, [DFF, S]) for i in range(2)]
    upT = _Rows(upC, DFF)
    fT = scratch("fT", [DFF, S], BF16)
    dT = scratch("dT", [D, S])

    WT_G = 64
    wt_pool = [nc.dram_tensor("wt%d" % i, [WT_G, 128, 8192], BF16, kind="Internal").ap() for i in range(3)]

    def wt_ap(gi, n):
        return wt_pool[gi // WT_G][gi % WT_G, :, 0:n]

    es = ExitStack()
    with es:
        def sb(name, shape, dt=F32):
            return es.enter_context(nc.sbuf_tensor(name, list(shape), dt))

        def ps(name, shape, dt=F32):
            return es.enter_context(nc.psum_tensor(name, list(shape), dt))

        AFW = 22016
        ABW = 40960
        arenaF = sb("arenaF", [128, AFW], F32)
        arenaB = sb("arenaB", [128, ABW], BF16)
        cst = sb("cst", [128, 640], F32)
        identb = sb("identb", [128, 128], BF16)
        onesb = sb("onesb", [128, 128], BF16)
        NV = 6 * KC * (1 + L) + 6 * KC + 8 * KC * L
        vecs = sb("vecs", [128, 3584], F32)
        PB = [ps("pb%d" % i, [128, 512], F32) for i in range(6)]
        PBB = [ps("pbb%d" % i, [128, 1024], BF16) for i in range(2)]
        block = es.enter_context(nc.Block())
        P = Prog(nc)

        ident = cst[:, 0:128]
        ones = cst[:, 128:256]
        causal01 = cst[:, 256:384]
        negm = cst[:, 384:512]
        negt = cst[:, 512:640]

        class Arena:
            def __init__(self):
                self.f = 0
                self.b = 0

            def F(self, n):
                a = arenaF[:, self.f:self.f + n]
                self.f += n
                assert self.f <= AFW, ("arenaF overflow", self.f)
                return a

            def B(self, n):
                a = arenaB[:, self.b:self.b + n]
                self.b += n
                assert self.b <= ABW, ("arenaB overflow", self.b)
                return a

        P.dma(cst[:, :], consts[:, :], w=["cst"])
        P.op("act", lambda e: e.activation(out=identb[:, :], in_=ident, func=AF.Copy), r=["cst"], w=["identb"])
        P.op("act", lambda e: e.activation(out=onesb[:, :], in_=ones, func=AF.Copy), r=["cst"], w=["onesb"])
        CONSTS = ["cst", "identb", "onesb"]

        vpos = [0]

        def valloc(n):
            a = vecs[:, vpos[0]:vpos[0] + n]
            vpos[0] += n
            return a

        uid = [0]

        def vecT(src2d, R, dst):
            for r0 in range(0, R, 128):
                rr = min(128, R - r0)
                uid[0] += 1
                ar = Arena()
                st = ar.F(128)
                k = "vt%d" % uid[0]
                P.dma(st[0:rr, :], src2d[r0:r0 + rr, :], w=[k + "s"])
                P.op("pe", lambda e, rr=rr, st=st: e.transpose(out=PB[0][:, 0:rr], in_=st[0:rr, :],
                                                              identity=ident[0:rr, 0:rr]),
                     r=[k + "s", "cst"], w=["pb0"])
                P.op("dve", lambda e, rr=rr, r0=r0: e.tensor_copy(out=dst[:, r0:r0 + rr], in_=PB[0][:, 0:rr]),
                     r=["pb0"], w=["vecs"])
                P.barrier()

        def vec_rows(v1d, n):
            return v1d.rearrange("(r p) -> r p", p=128)

        P.barrier()
        cT = valloc(KC)
        vecT(vec_rows(c_in, D), KC, cT)
        scT = valloc(KC)
        P.op("act", lambda e: e.activation(out=scT, in_=cT, func=AF.Silu), r=["vecs"], w=["vecs"])
        P.barrier()
        NMB = 6 * KC
        badaT = valloc(NMB)
        for r0 in range(0, NMB, 96):
            rr = min(96, NMB - r0)
            vecT(vec_rows(b_ada, 6 * D)[r0:r0 + rr, :], rr, badaT[:, r0:r0 + rr])
        modS = valloc(NMB)
        CGM = 256
        ar = Arena()
        wst = [ar.F(KC * CGM) for _ in range(2)]
        ngr = (6 * D) // CGM
        for gi in range(ngr):
            w_ = wst[gi % 2].rearrange("p (k c) -> p k c", k=KC)
            kk = "wada%d" % (gi % 2)
            P.dma(w_, w_ada[:, gi * CGM:(gi + 1) * CGM].rearrange("(k p) c -> p k c", p=128), w=[kk])
            for sbk in range(CGM // 128):
                j = gi * (CGM // 128) + sbk
                for kc in range(KC):
                    P.op("pe", lambda e, w_=w_, sbk=sbk, kc=kc, j=j: e.matmul(
                        out=PB[1][:, j:j + 1], lhsT=w_[:, kc, sbk * 128:(sbk + 1) * 128], rhs=scT[:, kc:kc + 1],
                        start=(kc == 0), stop=(kc == KC - 1)), r=[kk, "vecs"], w=["pb1"])
        P.op("dve", lambda e: e.tensor_tensor(out=modS, in0=PB[1][:, 0:NMB], in1=badaT, op=ALU.add),
             r=["pb1", "vecs"], w=["vecs"])
        P.barrier()

        LV = []
        for l in range(L):
            d = {}
            tab = valloc(NMB)
            vecT(ada_table[l].rearrange("j (r p) -> (j r) p", p=128), NMB, tab)
            mod = valloc(NMB)
            P.op("dve", lambda e, mod=mod, tab=tab: e.tensor_tensor(out=mod, in0=modS, in1=tab, op=ALU.add),
                 r=["vecs"], w=["vecs"])
            P.barrier()
            d["shift1"], d["gate1"], d["shift2"], d["gate2"] = (mod[:, 0:KC], mod[:, 2 * KC:3 * KC],
                                                               mod[:, 3 * KC:4 * KC], mod[:, 5 * KC:6 * KC])
            for nm, gsrc, sidx in (("gs1", norm1_g, 1), ("gs2", norm2_g, 4)):
                gt = valloc(KC)
                vecT(vec_rows(gsrc[l], D), KC, gt)
                gs = valloc(KC)
                sc = mod[:, sidx * KC:(sidx + 1) * KC]
                P.op("dve", lambda e, gs=gs, sc=sc, gt=gt: e.scalar_tensor_tensor(
                    out=gs, in0=sc, scalar=1.0, in1=gt, op0=ALU.add, op1=ALU.mult), r=["vecs"], w=["vecs"])
                P.barrier()
                d[nm] = gs
            LV.append(d)

        def phase_tin():
            ar = Arena()
            xin = [ar.F(D) for _ in range(2)]
            xst = [ar.F(D) for _ in range(2)]
            for i in range(NB):
                b = i % 2
                P.dma(xin[b], x_in[i * 128:(i + 1) * 128, :], w=["xin%d" % b])
                xs3 = xst[b].rearrange("p (k t) -> p k t", k=KC)
                for k4 in range(0, KC, 4):
                    pbk = "pb%d" % ((k4 // 4) % 2)
                    pb = PB[(k4 // 4) % 2]
                    for q in range(4):
                        P.op("pe", lambda e, pb=pb, q=q, b=b, k4=k4: e.transpose(
                            out=pb[:, q * 128:(q + 1) * 128], in_=xin[b][:, (k4 + q) * 128:(k4 + q + 1) * 128],
                            identity=ident), r=["xin%d" % b, "cst"], w=[pbk])
                    P.op("dve", lambda e, pb=pb, b=b, k4=k4: e.tensor_copy(
                        out=xst[b][:, k4 * 128:(k4 + 4) * 128], in_=pb[:, :]), r=[pbk], w=["xst%d" % b])
                P.dma(xT[:, i * 128:(i + 1) * 128].rearrange("(k p) t -> p k t", p=128), xs3, r=["xst%d" % b])
            P.barrier()

        def phase_norm(resid, gate, gs, shift):
            TN = 256
            ar = Arena()
            xt = ar.F(KC * TN)
            pt = ar.F(KC * TN)
            sq = [ar.F(TN) for _ in range(2)]
            rstd = ar.F(TN)
            tmp = [ar.F(TN) for _ in range(2)]
            hb = ar.B(KC * TN)
            x3 = xt.rearrange("p (k t) -> p k t", k=KC)
            p3 = pt.rearrange("p (k t) -> p k t", k=KC)
            h3 = hb.rearrange("p (k t) -> p k t", k=KC)
            for tt in range(S // TN):
                sl = slice(tt * TN, (tt + 1) * TN)
                P.dma(x3, xT[:, sl].rearrange("(k p) t -> p k t", p=128), w=["xt"])
                if resid is not None:
                    P.dma(p3, resid[:, sl].rearrange("(k p) t -> p k t", p=128), w=["pt"])
                    for kc in range(KC):
                        P.op("dve", lambda e, kc=kc: e.scalar_tensor_tensor(
                            out=x3[:, kc, :], in0=p3[:, kc, :], scalar=gate[:, kc:kc + 1], in1=x3[:, kc, :],
                            op0=ALU.mult, op1=ALU.add), r=["xt", "pt", "vecs"], w=["xt"])
                    P.dma(xT[:, sl].rearrange("(k p) t -> p k t", p=128), x3, r=["xt"])
                for kc in range(KC):
                    b = kc % 2
                    P.op("act", lambda e, kc=kc, b=b: e.activation(out=sq[b], in_=x3[:, kc, :], func=AF.Square),
                         r=["xt"], w=["sq%d" % b])
                    P.op("pe", lambda e, kc=kc, b=b: e.matmul(out=PB[0][:, 0:TN], lhsT=ones, rhs=sq[b],
                                                              start=(kc == 0), stop=(kc == KC - 1)),
                         r=["sq%d" % b, "cst"], w=["pb0"])
                P.op("act", lambda e: e.activation(out=rstd, in_=PB[0][:, 0:TN], func=AF.Sqrt, scale=1.0 / D, bias=EPS),
                     r=["pb0"], w=["rstd"])
                P.op("dve", lambda e: e.reciprocal(out=rstd, in_=rstd), r=["rstd"], w=["rstd"])
                for kc in range(KC):
                    b = kc % 2
                    P.op("dve", lambda e, kc=kc, b=b: e.scalar_tensor_tensor(
                        out=tmp[b], in0=x3[:, kc, :], scalar=gs[:, kc:kc + 1], in1=rstd, op0=ALU.mult, op1=ALU.mult),
                        r=["xt", "rstd", "vecs"], w=["tmp%d" % b])
                    P.op("act", lambda e, kc=kc, b=b: e.activation(
                        out=h3[:, kc, :], in_=tmp[b], func=AF.Identity, bias=shift[:, kc:kc + 1]),
                        r=["tmp%d" % b, "vecs"], w=["hb"])
                P.dma(hT[:, sl].rearrange("(k p) t -> p k t", p=128), h3, r=["hb"])
            P.barrier()

        def phase_gemm(src, K, W, NCOL, dst):
            KCk = K // 128
            CG = 256 if KCk <= 32 else 64
            ar = Arena()
            wf = [ar.F(KCk * CG) for _ in range(2)]
            wc = [ar.B(KCk * CG) for _ in range(2)]
            groups = []
            for gi, c0 in enumerate(range(0, NCOL, CG)):
                cg = min(CG, NCOL - c0)
                groups.append((gi, c0, cg))
                b = gi % 2
                wf3 = wf[b][:, 0:KCk * cg].rearrange("p (k c) -> p k c", k=KCk)
                wc3 = wc[b][:, 0:KCk * cg].rearrange("p (k c) -> p k c", k=KCk)
                P.dma(wf3, W[:, c0:c0 + cg].rearrange("(k p) c -> p k c", p=128), w=["wf%d" % b])
                q1 = max(1, KCk // 2)
                q2 = max(q1, (3 * KCk) // 4)
                parts = [("dve", 0, q1, "wcA%d" % b), ("act", q1, q2, "wcB%d" % b), ("pool", q2, KCk, "wcC%d" % b)]
                keys = []
                for (eng, k0, k1, key) in parts:
                    if k1 <= k0:
                        continue
                    keys.append(key)
                    if eng == "act":
                        P.op("act", lambda e, wc3=wc3, wf3=wf3, k0=k0, k1=k1: e.activation(
                            out=wc3[:, k0:k1, :], in_=wf3[:, k0:k1, :], func=AF.Copy), r=["wf%d" % b], w=[key])
                    else:
                        P.op(eng, lambda e, wc3=wc3, wf3=wf3, k0=k0, k1=k1: e.tensor_copy(
                            out=wc3[:, k0:k1, :], in_=wf3[:, k0:k1, :]), r=["wf%d" % b], w=[key])
                P.dma(wt_ap(gi, KCk * cg), wc[b][:, 0:KCk * cg], r=keys)
            P.barrier()
            ar = Arena()
            act = ar.B(KCk * 512)
            a3 = act.rearrange("p (k t) -> p k t", k=KCk)
            NWB = 3 if KCk <= 32 else 2
            wb = [ar.B(KCk * CG) for _ in range(NWB)]
            ost = [ar.F(512) for _ in range(2)]
            n = 0
            m = 0
            for tt in range(NT):
                sl = slice(tt * 512, (tt + 1) * 512)
                P.dma(a3, src[:, sl].rearrange("(k p) t -> p k t", p=128), w=["act"])
                for (gi, c0, cg) in groups:
                    b = n % NWB
                    n += 1
                    wb3 = wb[b][:, 0:KCk * cg].rearrange("p (k c) -> p k c", k=KCk)
                    P.dma(wb[b][:, 0:KCk * cg], wt_ap(gi, KCk * cg), w=["wb%d" % b])
                    for s0 in range(0, cg, 128):
                        mm = min(128, cg - s0)
                        pi = 2 + (m % 2)
                        ob = m % 2
                        m += 1
                        for kc in range(KCk):
                            P.op("pe", lambda e, wb3=wb3, kc=kc, s0=s0, mm=mm, pi=pi: e.matmul(
                                out=PB[pi][0:mm, :], lhsT=wb3[:, kc, s0:s0 + mm], rhs=a3[:, kc, :],
                                start=(kc == 0), stop=(kc == KCk - 1)),
                                r=["wb%d" % b, "act"], w=["pb%d" % pi])
                        P.op("act", lambda e, mm=mm, pi=pi, ob=ob: e.activation(
                            out=ost[ob][0:mm, :], in_=PB[pi][0:mm, :], func=AF.Copy),
                            r=["pb%d" % pi], w=["ost%d" % ob])
                        P.dma(dst[c0 + s0:c0 + s0 + mm, sl], ost[ob][0:mm, :], r=["ost%d" % ob])
            P.barrier()

        def phase_ssdprep(l):
            NBLK = CONVCH // 128
            cw = valloc(4 * NBLK)
            vecT(ssd_conv_w[l].rearrange("k (r p) -> (k r) p", p=128)[0:min(128, 4 * NBLK), :], min(128, 4 * NBLK),
                 cw[:, 0:min(128, 4 * NBLK)])
            r0 = 128
            while r0 < 4 * NBLK:
                rr = min(128, 4 * NBLK - r0)
                vecT(ssd_conv_w[l].rearrange("k (r p) -> (k r) p", p=128)[r0:r0 + rr, :], rr, cw[:, r0:r0 + rr])
                r0 += rr
            cb = valloc(NBLK)
            vecT(vec_rows(ssd_conv_b[l], CONVCH), NBLK, cb)
            hv = valloc(4)
            P.dma(hv[0:H, 0:1], ssd_dt_bias[l].rearrange("(p o) -> p o", o=1), w=["vecs"])
            P.dma(hv[0:H, 1:2], ssd_a_log[l].rearrange("(p o) -> p o", o=1), w=["vecs"])
            P.barrier()
            P.op("act", lambda e: e.activation(out=hv[0:H, 2:3], in_=hv[0:H, 1:2], func=AF.Exp), r=["vecs"], w=["vecs"])
            P.barrier()
            P.op("dve", lambda e: e.tensor_scalar(out=hv[0:H, 2:3], in0=hv[0:H, 2:3], scalar1=-1.0, scalar2=None,
                                                  op0=ALU.mult), r=["vecs"], w=["vecs"])
            P.barrier()
            ar = Arena()
            xr = [ar.F(516) for _ in range(2)]
            acc = [ar.F(512) for _ in range(2)]
            ob = [ar.B(512) for _ in range(2)]
            dtr = ar.F(512)
            dtv = ar.F(512)
            dav = ar.F(512)
            acv = ar.F(512)
            n = 0
            for tt in range(NT):
                t0 = tt * 512
                for blk in range(NBLK):
                    b = n % 2
                    n += 1
                    rows = slice(cfg.o_xbc + blk * 128, cfg.o_xbc + (blk + 1) * 128)
                    if tt == 0:
                        P.op("pool", lambda e, b=b: e.memset(xr[b][:, 0:3], 0.0), w=["xr%d" % b])
                        P.dma(xr[b][:, 3:515], projT[rows, 0:512], w=["xr%d" % b])
                    else:
                        P.dma(xr[b][:, 0:515], projT[rows, t0 - 3:t0 + 512], w=["xr%d" % b])
                    for k in range(4):
                        wk = cw[:, k * NBLK + blk:k * NBLK + blk + 1]
                        if k == 0:
                            P.op("dve", lambda e, b=b, wk=wk: e.tensor_scalar(
                                out=acc[b], in0=xr[b][:, 0:512], scalar1=wk, scalar2=None, op0=ALU.mult),
                                r=["xr%d" % b, "vecs"], w=["acc%d" % b])
                        else:
                            P.op("dve", lambda e, b=b, wk=wk, k=k: e.scalar_tensor_tensor(
                                out=acc[b], in0=xr[b][:, k:k + 512], scalar=wk, in1=acc[b], op0=ALU.mult, op1=ALU.add),
                                r=["xr%d" % b, "vecs", "acc%d" % b], w=["acc%d" % b])
                    P.op("act", lambda e, b=b, blk=blk: e.activation(out=ob[b], in_=acc[b], func=AF.Silu,
                                                                     bias=cb[:, blk:blk + 1]),
                         r=["acc%d" % b, "vecs"], w=["ob%d" % b])
                    P.dma(xbcT[blk * 128:(blk + 1) * 128, t0:t0 + 512], ob[b], r=["ob%d" % b])
                P.dma(dtr[0:H, :], projT[cfg.o_dt:cfg.o_dt + H, t0:t0 + 512], w=["dtr"])
                P.op("act", lambda e: e.activation(out=dtv[0:H, :], in_=dtr[0:H, :], func=AF.Exp, bias=hv[0:H, 0:1]),
                     r=["dtr", "vecs"], w=["dtv"])
                P.op("act", lambda e: e.activation(out=dtv[0:H, :], in_=dtv[0:H, :], func=AF.Ln, bias=1.0),
                     r=["dtv"], w=["dtv"])
                P.op("dve", lambda e: e.tensor_scalar(out=dav[0:H, :], in0=dtv[0:H, :], scalar1=hv[0:H, 2:3],
                                                      scalar2=None, op0=ALU.mult), r=["dtv", "vecs"], w=["dav"])
                for c in range(4):
                    P.op("dve", lambda e, c=c: e.tensor_tensor_scan(
                        out=acv[0:H, c * 128:(c + 1) * 128], data0=ones[0:H, :], data1=dav[0:H, c * 128:(c + 1) * 128],
                        initial=0.0, op0=ALU.mult, op1=ALU.add), r=["dav", "cst"], w=["acv"])
                P.dma(dtT[:, t0:t0 + 512], dtv[0:H, :], r=["dtv"])
                P.dma(acumT[:, t0:t0 + 512], acv[0:H, :], r=["acv"])
            P.barrier()

        def phase_ssdmain(l):
            NXB = DI // 128
            ar = Arena()
            dskr = ar.F(H)
            dsk = ar.F(H)
            P.dma(dskr[0:1, :], ssd_d[l].rearrange("(o h) -> o h", o=1), w=["dskr"])
            P.op("pe", lambda e: e.matmul(out=PB[0][:, 0:H], lhsT=ones[0:1, :], rhs=dskr[0:1, :], start=True, stop=True),
                 r=["dskr", "cst"], w=["pb0"])
            P.op("dve", lambda e: e.tensor_copy(out=dsk, in_=PB[0][:, 0:H]), r=["pb0"], w=["dsk"])
            stT = ar.F(DI)
            stTb = ar.B(DI)
            P.op("pool", lambda e: e.memset(stT, 0.0), w=["stT"])
            P.op("pool", lambda e: e.memset(stTb, 0.0), w=["stTb"])
            xsf = [ar.B(1024) for _ in range(2)]
            xtok = ar.B(DI)
            xdt = ar.B(DI)
            wxg = [ar.B(512) for _ in range(2)]
            bTt = ar.B(G * 128)
            cTt = ar.B(G * 128)
            btok = ar.B(G * 128)
            dtc = ar.F(128)
            acT = ar.F(128)
            dttok = ar.F(H)
            actok = ar.F(H)
            nactok = ar.F(H)
            lastbc = ar.F(H)
            explast = ar.F(H)
            wtok = ar.F(H)
            eactok = ar.F(H)
            cbm = [ar.F(128) for _ in range(2)]
            Dh = [ar.F(128) for _ in range(2)]
            Mh = [ar.B(128) for _ in range(2)]
            ytok = ar.F(DI)
            ttmp = ar.F(GW)
            yst = [ar.F(512) for _ in range(2)]
            bT3 = bTt.rearrange("p (g t) -> p g t", g=G)
            cT3 = cTt.rearrange("p (g t) -> p g t", g=G)
            bk3 = btok.rearrange("p (g n) -> p g n", g=G)

            def hb(ap, h0, nh):
                return ap[:, h0:h0 + nh].unsqueeze(2).to_broadcast([128, nh, 64])

            def v3(ap, h0, nh):
                return ap[:, h0 * 64:(h0 + nh) * 64].rearrange("p (h d) -> p h d", d=64)

            nE = 0
            nW = [0]
            nY = 0
            for c in range(NB):
                t0 = c * 128
                ts = slice(t0, t0 + 128)
                P.dma(bT3, xbcT[DI:DI + G * 128, ts].rearrange("(g p) t -> p g t", p=128), w=["bT"])
                P.dma(cT3, xbcT[DI + G * 128:DI + 2 * G * 128, ts].rearrange("(g p) t -> p g t", p=128), w=["cT"])
                P.dma(dtc[0:H, :], dtT[:, ts], w=["dtc"])
                P.dma(acT[0:H, :], acumT[:, ts], w=["acT"])
                for k8 in range(0, NXB, 8):
                    nb8 = min(8, NXB - k8)
                    xb_ = (k8 // 8) % 2
                    pbb = PBB[xb_]
                    pk = "pbb%d" % xb_
                    xs3 = xsf[xb_][:, 0:nb8 * 128].rearrange("p (k t) -> p k t", t=128)
                    P.dma(xs3, xbcT[k8 * 128:(k8 + nb8) * 128, ts].rearrange("(k p) t -> p k t", p=128),
                          w=["xsf%d" % xb_])
                    for q in range(nb8):
                        P.op("pe", lambda e, pbb=pbb, q=q, xs3=xs3: e.transpose(
                            out=pbb[:, q * 128:(q + 1) * 128], in_=xs3[:, q, :], identity=identb[:, :]),
                            r=["xsf%d" % xb_, "identb"], w=[pk])
                    P.op("act", lambda e, pbb=pbb, k8=k8, nb8=nb8: e.activation(
                        out=xtok[:, k8 * 128:(k8 + nb8) * 128], in_=pbb[:, 0:nb8 * 128], func=AF.Copy),
                        r=[pk], w=["xtok"])
                for g in range(G):
                    P.op("pe", lambda e, g=g: e.transpose(out=PBB[0][:, g * 128:(g + 1) * 128], in_=bT3[:, g, :],
                                                          identity=identb[:, :]), r=["bT", "identb"], w=["pbb0"])
                P.op("act", lambda e: e.activation(out=btok, in_=PBB[0][:, 0:G * 128], func=AF.Copy),
                     r=["pbb0"], w=["btok"])
                P.op("pe", lambda e: e.transpose(out=PB[0][:, 0:H], in_=dtc[0:H, :], identity=ident[0:H, 0:H]),
                     r=["dtc", "cst"], w=["pb0"])
                P.op("dve", lambda e: e.tensor_copy(out=dttok, in_=PB[0][:, 0:H]), r=["pb0"], w=["dttok"])
                P.op("pe", lambda e: e.transpose(out=PB[1][:, 0:H], in_=acT[0:H, :], identity=ident[0:H, 0:H]),
                     r=["acT", "cst"], w=["pb1"])
                P.op("dve", lambda e: e.tensor_copy(out=actok, in_=PB[1][:, 0:H]), r=["pb1"], w=["actok"])
                P.op("dve", lambda e: e.tensor_scalar(out=nactok, in0=actok, scalar1=-1.0, scalar2=None, op0=ALU.mult),
                     r=["actok"], w=["nactok"])
                P.op("pe", lambda e: e.matmul(out=PB[0][:, 0:H], lhsT=ident[:, 127:128].to_broadcast([128, 128]),
                                              rhs=actok, start=True, stop=True), r=["actok", "cst"], w=["pb0"])
                P.op("dve", lambda e: e.tensor_copy(out=lastbc, in_=PB[0][:, 0:H]), r=["pb0"], w=["lastbc"])
                P.op("act", lambda e: e.activation(out=explast, in_=lastbc, func=AF.Exp), r=["lastbc"], w=["explast"])
                P.op("dve", lambda e: e.tensor_tensor(out=wtok, in0=lastbc, in1=actok, op=ALU.subtract),
                     r=["lastbc", "actok"], w=["wtok"])
                P.op("act", lambda e: e.activation(out=wtok, in_=wtok, func=AF.Exp), r=["wtok"], w=["wtok"])
                P.op("act", lambda e: e.activation(out=eactok, in_=actok, func=AF.Exp), r=["actok"], w=["eactok"])
                P.op("dve", lambda e: e.tensor_tensor(out=v3(xdt, 0, H), in0=v3(xtok, 0, H), in1=hb(dttok, 0, H),
                                                      op=ALU.mult), r=["xtok", "dttok"], w=["xdt"])
                for g in range(G):
                    h0 = g * HG
                    cbk = g % 2
                    P.op("pe", lambda e, g=g: e.matmul(out=PB[1][:, 0:128], lhsT=bT3[:, g, :], rhs=cT3[:, g, :],
                                                       start=True, stop=True), r=["bT", "cT"], w=["pb1"])
                    P.op("dve", lambda e, cbk=cbk: e.tensor_tensor(out=cbm[cbk], in0=PB[1][:, 0:128], in1=causal01,
                                                                   op=ALU.mult), r=["pb1", "cst"], w=["cbm%d" % cbk])
                    for w0 in range(0, GW, 512):
                        ww = min(512, GW - w0)
                        hh0 = h0 + w0 // 64
                        nh = ww // 64
                        c0 = g * GW + w0
                        P.op("pe", lambda e, g=g, c0=c0, ww=ww: e.matmul(
                            out=PB[2][:, 0:ww], lhsT=cT3[:, g, :], rhs=stTb[:, c0:c0 + ww], start=True, stop=True),
                            r=["cT", "stTb"], w=["pb2"])
                        P.op("dve", lambda e, hh0=hh0, nh=nh, ww=ww: e.tensor_tensor(
                            out=v3(ytok, hh0, nh), in0=PB[2][:, 0:ww].rearrange("p (h d) -> p h d", d=64),
                            in1=hb(eactok, hh0, nh), op=ALU.mult), r=["pb2", "eactok"], w=["ytok"])
                        for hi in range(nh):
                            h = hh0 + hi
                            eb = nE % 2
                            nE += 1
                            pe_ = PB[4 + eb]
                            pek = "pb%d" % (4 + eb)
                            P.op("pe", lambda e, h=h, pe_=pe_: e.matmul(
                                out=pe_[:, 0:128], lhsT=ident[0:H, h:h + 1].to_broadcast([H, 128]), rhs=acT[0:H, :],
                                start=True, stop=False), r=["acT", "cst"], w=[pek])
                            P.op("pe", lambda e, pe_=pe_: e.matmul(out=pe_[:, 0:128], lhsT=ident, rhs=negm,
                                                                   start=False, stop=True), r=["cst"], w=[pek])
                            P.op("act", lambda e, h=h, pe_=pe_, eb=eb: e.activation(
                                out=Dh[eb], in_=pe_[:, 0:128], func=AF.Exp, bias=nactok[:, h:h + 1]),
                                r=[pek, "nactok"], w=["Dh%d" % eb])
                            P.op("dve", lambda e, eb=eb, cbk=cbk: e.tensor_tensor(
                                out=Mh[eb], in0=Dh[eb], in1=cbm[cbk], op=ALU.mult),
                                r=["Dh%d" % eb, "cbm%d" % cbk], w=["Mh%d" % eb])
                            P.op("pe", lambda e, eb=eb, h=h, hi=hi: e.matmul(
                                out=PB[3][:, hi * 64:(hi + 1) * 64], lhsT=Mh[eb], rhs=xdt[:, h * 64:(h + 1) * 64],
                                start=True, stop=True), r=["Mh%d" % eb, "xdt"], w=["pb3"])
                        P.op("dve", lambda e, c0=c0, ww=ww: e.tensor_tensor(
                            out=ytok[:, c0:c0 + ww], in0=PB[3][:, 0:ww], in1=ytok[:, c0:c0 + ww], op=ALU.add),
                            r=["pb3", "ytok"], w=["ytok"])
                        P.op("pool", lambda e, hh0=hh0, nh=nh, ww=ww: e.tensor_tensor(
                            out=ttmp[:, 0:ww].rearrange("p (h d) -> p h d", d=64), in0=v3(xtok, hh0, nh),
                            in1=hb(dsk, hh0, nh), op=ALU.mult), r=["xtok", "dsk"], w=["ttmp"])
                        P.op("dve", lambda e, c0=c0, ww=ww: e.tensor_tensor(
                            out=ytok[:, c0:c0 + ww], in0=ttmp[:, 0:ww], in1=ytok[:, c0:c0 + ww], op=ALU.add),
                            r=["ttmp", "ytok"], w=["ytok"])
                        wb_ = nW[0] % 2
                        nW[0] += 1
                        P.op("pool", lambda e, wb_=wb_, hh0=hh0, nh=nh, ww=ww: e.tensor_tensor(
                            out=wxg[wb_][:, 0:ww].rearrange("p (h d) -> p h d", d=64), in0=v3(xdt, hh0, nh),
                            in1=hb(wtok, hh0, nh), op=ALU.mult), r=["xdt", "wtok"], w=["wx%d" % wb_])
                        P.op("pe", lambda e, g=g, wb_=wb_, ww=ww: e.matmul(
                            out=PB[2][:, 0:ww], lhsT=bk3[:, g, :], rhs=wxg[wb_][:, 0:ww], start=True, stop=True),
                            r=["btok", "wx%d" % wb_], w=["pb2"])
                        P.op("pool", lambda e, hh0=hh0, nh=nh: e.tensor_tensor(
                            out=v3(stT, hh0, nh), in0=v3(stT, hh0, nh), in1=hb(explast, hh0, nh), op=ALU.mult),
                            r=["stT", "explast"], w=["stT"])
                        P.op("dve", lambda e, c0=c0, ww=ww: e.tensor_tensor(
                            out=stT[:, c0:c0 + ww], in0=PB[2][:, 0:ww], in1=stT[:, c0:c0 + ww], op=ALU.add),
                            r=["pb2", "stT"], w=["stT"])
                        P.op("act", lambda e, c0=c0, ww=ww: e.activation(
                            out=stTb[:, c0:c0 + ww], in_=stT[:, c0:c0 + ww], func=AF.Copy), r=["stT"], w=["stTb"])
                for k4 in range(0, NXB, 4):
                    nb4 = min(4, NXB - k4)
                    pi = (k4 // 4) % 2
                    yb_ = nY % 2
                    nY += 1
                    for q in range(nb4):
                        P.op("pe", lambda e, pi=pi, q=q, k4=k4: e.transpose(
                            out=PB[pi][:, q * 128:(q + 1) * 128], in_=ytok[:, (k4 + q) * 128:(k4 + q + 1) * 128],
                            identity=ident), r=["ytok", "cst"], w=["pb%d" % pi])
                    P.op("act", lambda e, pi=pi, yb_=yb_, nb4=nb4: e.activation(
                        out=yst[yb_][:, 0:nb4 * 128], in_=PB[pi][:, 0:nb4 * 128], func=AF.Copy),
                        r=["pb%d" % pi], w=["yst%d" % yb_])
                    P.dma(yT[k4 * 128:(k4 + nb4) * 128, ts].rearrange("(k p) t -> p k t", p=128),
                          yst[yb_][:, 0:nb4 * 128].rearrange("p (k t) -> p k t", t=128), r=["yst%d" % yb_])
            P.barrier()

        def phase_ssdpost(l):
            GB = GW // 128
            NXB = DI // 128
            ng = valloc(NXB)
            vecT(vec_rows(ssd_norm_g[l], DI), NXB, ng)
            TN = 256
            ar = Arena()
            yb = ar.F(GB * TN)
            zb = ar.F(GB * TN)
            sq = [ar.F(TN) for _ in range(2)]
            rstd = ar.F(TN)
            tmp = [ar.F(TN) for _ in range(2)]
            onb = ar.B(GB * TN)
            y3 = yb.rearrange("p (k t) -> p k t", k=GB)
            z3 = zb.rearrange("p (k t) -> p k t", k=GB)
            o3 = onb.rearrange("p (k t) -> p k t", k=GB)
            for tt in range(S // TN):
                sl = slice(tt * TN, (tt + 1) * TN)
                for g in range(G):
                    rows = slice(g * GW, (g + 1) * GW)
                    P.dma(y3, yT[rows, sl].rearrange("(k p) t -> p k t", p=128), w=["yb"])
                    P.dma(z3, projT[cfg.o_z + g * GW:cfg.o_z + (g + 1) * GW, sl].rearrange("(k p) t -> p k t", p=128),
                          w=["zb"])
                    P.op("act", lambda e: e.activation(out=zb, in_=zb, func=AF.Silu), r=["zb"], w=["zb"])
                    P.op("dve", lambda e: e.tensor_tensor(out=yb, in0=yb, in1=zb, op=ALU.mult), r=["yb", "zb"], w=["yb"])
                    for k in range(GB):
                        b = k % 2
                        P.op("act", lambda e, k=k, b=b: e.activation(out=sq[b], in_=y3[:, k, :], func=AF.Square),
                             r=["yb"], w=["sq%d" % b])
                        P.op("pe", lambda e, k=k, b=b: e.matmul(out=PB[0][:, 0:TN], lhsT=ones, rhs=sq[b],
                                                                start=(k == 0), stop=(k == GB - 1)),
                             r=["sq%d" % b, "cst"], w=["pb0"])
                    P.op("act", lambda e: e.activation(out=rstd, in_=PB[0][:, 0:TN], func=AF.Sqrt, scale=1.0 / GW,
                                                       bias=EPS), r=["pb0"], w=["rstd"])
                    P.op("dve", lambda e: e.reciprocal(out=rstd, in_=rstd), r=["rstd"], w=["rstd"])
                    for k in range(GB):
                        b = k % 2
                        kb = g * GB + k
                        P.op("dve", lambda e, k=k, b=b, kb=kb: e.scalar_tensor_tensor(
                            out=o3[:, k, :], in0=y3[:, k, :], scalar=ng[:, kb:kb + 1], in1=rstd, op0=ALU.mult,
                            op1=ALU.mult), r=["yb", "rstd", "vecs"], w=["onb"])
                    P.dma(ynT[rows, sl].rearrange("(k p) t -> p k t", p=128), o3, r=["onb"])
            P.barrier()

        def phase_dsaprep(l):
            gv = valloc(4)
            P.dma(gv[:, 0:1], q_norm_g[l].rearrange("(p o) -> p o", o=1), w=["vecs"])
            P.dma(gv[:, 1:2], k_norm_g[l].rearrange("(p o) -> p o", o=1), w=["vecs"])
            P.dma(gv[:, 2:3], idx_k_norm_g[l].rearrange("(p o) -> p o", o=1), w=["vecs"])
            P.barrier()
            P.op("dve", lambda e: e.tensor_scalar(out=gv[:, 0:1], in0=gv[:, 0:1], scalar1=float(128 ** -0.5),
                                                  scalar2=None, op0=ALU.mult), r=["vecs"], w=["vecs"])
            P.barrier()
            ar = Arena()
            qb = [ar.F(512) for _ in range(2)]
            sq = [ar.F(512) for _ in range(2)]
            rstd = [ar.F(512) for _ in range(2)]
            ob = [ar.B(512) for _ in range(2)]
            vst = [ar.B(512) for _ in range(2)]
            wir = ar.F(512)
            wst = ar.F(128)
            n = 0
            for tt in range(NT):
                t0 = tt * 512
                sl = slice(t0, t0 + 512)
                jobs = [(cfg.o_q + i * 128, qnT, i * 128, 0) for i in range(ATT // 128)]
                jobs += [(cfg.o_k + i * 128, knT, i * 128, 1) for i in range(KV // 128)]
                jobs += [(cfg.o_ki, kinT, 0, 2)]
                for (srow, dstt, drow, gi) in jobs:
                    b = n % 2
                    n += 1
                    P.dma(qb[b], projT[srow:srow + 128, sl], w=["qb%d" % b])
                    P.op("act", lambda e, b=b: e.activation(out=sq[b], in_=qb[b], func=AF.Square),
                         r=["qb%d" % b], w=["sq%d" % b])
                    P.op("pe", lambda e, b=b: e.matmul(out=PB[b][:, :], lhsT=ones, rhs=sq[b], start=True, stop=True),
                         r=["sq%d" % b, "cst"], w=["pb%d" % b])
                    P.op("act", lambda e, b=b: e.activation(out=rstd[b], in_=PB[b][:, :], func=AF.Sqrt,
                                                            scale=1.0 / 128, bias=EPS), r=["pb%d" % b], w=["rstd%d" % b])
                    P.op("dve", lambda e, b=b: e.reciprocal(out=rstd[b], in_=rstd[b]), r=["rstd%d" % b],
                         w=["rstd%d" % b])
                    P.op("dve", lambda e, b=b, gi=gi: e.scalar_tensor_tensor(
                        out=ob[b], in0=qb[b], scalar=gv[:, gi:gi + 1], in1=rstd[b], op0=ALU.mult, op1=ALU.mult),
                        r=["qb%d" % b, "rstd%d" % b, "vecs"], w=["ob%d" % b])
                    P.dma(dstt[drow:drow + 128, sl], ob[b], r=["ob%d" % b])
                for i in range(IDX // 128):
                    b = n % 2
                    n += 1
                    P.dma(qb[b], projT[cfg.o_qi + i * 128:cfg.o_qi + (i + 1) * 128, sl], w=["qb%d" % b])
                    P.op("act", lambda e, b=b: e.activation(out=ob[b], in_=qb[b], func=AF.Copy,
                                                            scale=float(128 ** -0.5)), r=["qb%d" % b], w=["ob%d" % b])
                    P.dma(qiT[i * 128:(i + 1) * 128, sl], ob[b], r=["ob%d" % b])
                for g in range(KV // 128):
                    b = n % 2
                    n += 1
                    P.dma(qb[b], projT[cfg.o_v + g * 128:cfg.o_v + (g + 1) * 128, sl], w=["qb%d" % b])
                    for q in range(4):
                        P.op("pe", lambda e, b=b, q=q: e.transpose(out=PB[2 + b][:, q * 128:(q + 1) * 128],
                                                                   in_=qb[b][:, q * 128:(q + 1) * 128], identity=ident),
                             r=["qb%d" % b, "cst"], w=["pb%d" % (2 + b)])
                    P.op("act", lambda e, b=b: e.activation(out=vst[b], in_=PB[2 + b][:, :], func=AF.Copy),
                         r=["pb%d" % (2 + b)], w=["vst%d" % b])
                    P.dma(vtok[t0:t0 + 512, g * 128:(g + 1) * 128].rearrange("(q p) d -> p q d", p=128),
                          vst[b].rearrange("p (q d) -> p q d", q=4), r=["vst%d" % b])
                P.dma(wir[0:32, :], projT[cfg.o_wi:cfg.o_wi + 32, sl], w=["wir"])
                for q in range(4):
                    P.op("pe", lambda e, q=q: e.transpose(out=PB[4][:, q * 32:(q + 1) * 32],
                                                          in_=wir[0:32, q * 128:(q + 1) * 128], identity=ident[0:32, 0:32]),
                         r=["wir", "cst"], w=["pb4"])
                P.op("act", lambda e: e.activation(out=wst, in_=PB[4][:, 0:128], func=AF.Copy, scale=float(32 ** -0.5)),
                     r=["pb4"], w=["wst"])
                P.dma(witok[t0:t0 + 512, :].rearrange("(q p) h -> p q h", p=128),
                      wst.rearrange("p (q h) -> p q h", q=4), r=["wst"])
            P.barrier()

        def phase_dsaidx():
            ar = Arena()
            kin = ar.B(S)
            qib = [ar.B(32 * 128) for _ in range(2)]
            wi = [ar.F(32) for _ in range(2)]
            score = ar.F(S)
            work = ar.F(S)
            tmp = [ar.F(512) for _ in range(2)]
            m8 = ar.F(8)
            maskb = ar.B(S)
            mst = ar.B(S)
            P.dma(kin, kinT[:, :], w=["kin"])
            nps = 0
            for i in range(NB):
                b = i % 2
                nk = (i + 1) * 128
                q3 = qib[b].rearrange("p (h t) -> p h t", h=32)
                P.dma(q3, qiT[:, i * 128:(i + 1) * 128].rearrange("(h p) t -> p h t", p=128), w=["qib%d" % b])
                P.dma(wi[b], witok[i * 128:(i + 1) * 128, :], w=["wi%d" % b])
                for s0 in range(0, nk, 512):
                    sw = min(512, nk - s0)
                    for h in range(32):
                        pi = nps % 2
                        nps += 1
                        P.op("pe", lambda e, q3=q3, h=h, s0=s0, sw=sw, pi=pi: e.matmul(
                            out=PB[pi][:, 0:sw], lhsT=q3[:, h, :], rhs=kin[:, s0:s0 + sw], start=True, stop=True),
                            r=["qib%d" % b, "kin"], w=["pb%d" % pi])
                        if h == 0:
                            P.op("dve", lambda e, s0=s0, sw=sw, pi=pi, b=b: e.tensor_scalar(
                                out=score[:, s0:s0 + sw], in0=PB[pi][:, 0:sw], scalar1=0.0, scalar2=wi[b][:, 0:1],
                                op0=ALU.max, op1=ALU.mult), r=["pb%d" % pi, "wi%d" % b], w=["score"])
                        else:
                            tb = h % 2
                            P.op("dve", lambda e, sw=sw, pi=pi, b=b, h=h, tb=tb: e.tensor_scalar(
                                out=tmp[tb][:, 0:sw], in0=PB[pi][:, 0:sw], scalar1=0.0, scalar2=wi[b][:, h:h + 1],
                                op0=ALU.max, op1=ALU.mult), r=["pb%d" % pi, "wi%d" % b], w=["tmp%d" % tb])
                            P.op("pool", lambda e, s0=s0, sw=sw, tb=tb: e.tensor_tensor(
                                out=score[:, s0:s0 + sw], in0=score[:, s0:s0 + sw], in1=tmp[tb][:, 0:sw], op=ALU.add),
                                r=["tmp%d" % tb, "score"], w=["score"])
                P.op("dve", lambda e, i=i: e.tensor_tensor(out=score[:, i * 128:(i + 1) * 128],
                                                           in0=score[:, i * 128:(i + 1) * 128], in1=negt, op=ALU.add),
                     r=["score", "cst"], w=["score"])
                if nk > cfg.KSEL:
                    nr = cfg.KSEL // 8
                    for r_ in range(nr):
                        src = score if r_ == 0 else work
                        sk = "score" if r_ == 0 else "work"
                        P.op("dve", lambda e, src=src, nk=nk: e.max(out=m8, in_=src[:, 0:nk]), r=[sk], w=["m8"])
                        if r_ < nr - 1:
                            P.op("dve", lambda e, src=src, nk=nk: e.match_replace(
                                out=work[:, 0:nk], in_to_replace=m8, in_values=src[:, 0:nk], imm_value=-3.0e38),
                                r=[sk, "m8"], w=["work"])
                    P.op("dve", lambda e, nk=nk: e.tensor_scalar(out=maskb[:, 0:nk], in0=score[:, 0:nk],
                                                                 scalar1=m8[:, 7:8], scalar2=None, op0=ALU.is_ge),
                         r=["score", "m8"], w=["maskb"])
                else:
                    P.op("dve", lambda e, nk=nk: e.tensor_scalar(out=maskb[:, 0:nk], in0=score[:, 0:nk],
                                                                 scalar1=-1.0e29, scalar2=None, op0=ALU.is_ge),
                         r=["score"], w=["maskb"])
                nbt = 4 * (i // 4) + 4
                nbv = i + 1
                if nbt > nbv:
                    P.op("pool", lambda e, nbv=nbv, nbt=nbt: e.memset(mst[:, nbv * 128:nbt * 128], 0.0), w=["mst"])
                for k8 in range(0, nbv, 8):
                    n8 = min(8, nbv - k8)
                    pi = (k8 // 8) % 2
                    for q in range(n8):
                        P.op("pe", lambda e, pi=pi, q=q, k8=k8: e.transpose(
                            out=PBB[pi][:, q * 128:(q + 1) * 128], in_=maskb[:, (k8 + q) * 128:(k8 + q + 1) * 128],
                            identity=identb[:, :]), r=["maskb", "identb"], w=["pbb%d" % pi])
                    P.op("act", lambda e, pi=pi, k8=k8, n8=n8: e.activation(
                        out=mst[:, k8 * 128:(k8 + n8) * 128], in_=PBB[pi][:, 0:n8 * 128], func=AF.Copy),
                        r=["pbb%d" % pi], w=["mst"])
                P.dma(maskT[0:nbt * 128, i * 128:(i + 1) * 128].rearrange("(b p) t -> p b t", p=128),
                      mst[:, 0:nbt * 128].rearrange("p (b t) -> p b t", t=128), r=["mst"])
            P.barrier()

        def phase_dsaatt():
            ar = Arena()
            msk = ar.B(NB * 512)
            kg = ar.B(S)
            vg = ar.B(S)
            qh = [ar.B(512) for _ in range(2)]
            ee = [ar.B(512) for _ in range(2)]
            em = [ar.B(512) for _ in range(2)]
            rden = ar.F(512)
            osb = [ar.B(512) for _ in range(2)]
            nl = 0
            nh_ = 0
            for tg in range(NT):
                nkb = 4 * (tg + 1)
                sl = slice(tg * 512, (tg + 1) * 512)
                m3 = msk[:, 0:nkb * 512].rearrange("p (b t) -> p b t", t=512)
                P.dma(m3, maskT[0:nkb * 128, sl].rearrange("(b p) t -> p b t", p=128), w=["msk"])
                for g in range(4):
                    P.dma(kg[:, 0:nkb * 128], knT[g * 128:(g + 1) * 128, 0:nkb * 128], w=["kg"])
                    v3_ = vg[:, 0:nkb * 128].rearrange("p (b d) -> p b d", d=128)
                    P.dma(v3_, vtok[0:nkb * 128, g * 128:(g + 1) * 128].rearrange("(b p) d -> p b d", p=128), w=["vg"])
                    for hh in range(8):
                        h = g * 8 + hh
                        hb_ = nh_ % 2
                        nh_ += 1
                        pO = PB[2 + hb_]
                        pD = PB[4 + hb_]
                        P.dma(qh[hb_], qnT[h * 128:(h + 1) * 128, sl], w=["qh%d" % hb_])
                        for sbk in range(nkb):
                            lb = nl % 2
                            nl += 1
                            P.op("pe", lambda e, sbk=sbk, lb=lb, hb_=hb_: e.matmul(
                                out=PB[lb][:, :], lhsT=kg[:, sbk * 128:(sbk + 1) * 128], rhs=qh[hb_], start=True,
                                stop=True), r=["kg", "qh%d" % hb_], w=["pb%d" % lb])
                            P.op("act", lambda e, lb=lb: e.activation(out=ee[lb], in_=PB[lb][:, :], func=AF.Exp),
                                 r=["pb%d" % lb], w=["ee%d" % lb])
                            eng = "dve" if sbk % 2 == 0 else "pool"
                            P.op(eng, lambda e, lb=lb, sbk=sbk, m3=m3: e.tensor_tensor(
                                out=em[lb], in0=ee[lb], in1=m3[:, sbk, :], op=ALU.mult),
                                r=["ee%d" % lb, "msk"], w=["em%d" % lb])
                            P.op("pe", lambda e, sbk=sbk, lb=lb, pO=pO, v3_=v3_, nkb=nkb: e.matmul(
                                out=pO[:, :], lhsT=v3_[:, sbk, :], rhs=em[lb], start=(sbk == 0), stop=(sbk == nkb - 1)),
                                r=["vg", "em%d" % lb], w=["pb%d" % (2 + hb_)])
                            P.op("pe", lambda e, sbk=sbk, lb=lb, pD=pD, nkb=nkb: e.matmul(
                                out=pD[:, :], lhsT=onesb[:, :], rhs=em[lb], start=(sbk == 0), stop=(sbk == nkb - 1)),
                                r=["onesb", "em%d" % lb], w=["pb%d" % (4 + hb_)])
                        P.op("dve", lambda e, pD=pD: e.reciprocal(out=rden, in_=pD[:, :]), r=["pb%d" % (4 + hb_)],
                             w=["rden"])
                        P.op("dve", lambda e, pO=pO, hb_=hb_: e.tensor_tensor(out=osb[hb_], in0=pO[:, :], in1=rden,
                                                                              op=ALU.mult),
                             r=["pb%d" % (2 + hb_), "rden"], w=["osb%d" % hb_])
                        P.dma(oT[h * 128:(h + 1) * 128, sl], osb[hb_], r=["osb%d" % hb_])
            P.barrier()

        def phase_merge():
            ar = Arena()
            bufs = [[ar.F(512) for _ in range(4)] for _ in range(2)]
            ob = [ar.B(512) for _ in range(2)]
            n = 0
            for tt in range(NT):
                sl = slice(tt * 512, (tt + 1) * 512)
                for kc in range(KC):
                    b = n % 2
                    n += 1
                    ys, ya, gs_, ga_ = bufs[b]
                    rows = slice(kc * 128, (kc + 1) * 128)
                    P.dma(ys, yssdT[rows, sl], w=["m0%d" % b])
                    P.dma(ya, yattT[rows, sl], w=["m1%d" % b])
                    P.dma(gs_, projT[cfg.o_gs + kc * 128:cfg.o_gs + (kc + 1) * 128, sl], w=["m2%d" % b])
                    P.dma(ga_, projT[cfg.o_ga + kc * 128:cfg.o_ga + (kc + 1) * 128, sl], w=["m3%d" % b])
                    P.op("act", lambda e, gs_=gs_: e.activation(out=gs_, in_=gs_, func=AF.Sigmoid), r=["m2%d" % b],
                         w=["m2%d" % b])
                    P.op("act", lambda e, ga_=ga_: e.activation(out=ga_, in_=ga_, func=AF.Sigmoid), r=["m3%d" % b],
                         w=["m3%d" % b])
                    P.op("dve", lambda e, ys=ys, gs_=gs_: e.tensor_tensor(out=ys, in0=ys, in1=gs_, op=ALU.mult),
                         r=["m0%d" % b, "m2%d" % b], w=["m0%d" % b])
                    P.op("pool", lambda e, ya=ya, ga_=ga_: e.tensor_tensor(out=ya, in0=ya, in1=ga_, op=ALU.mult),
                         r=["m1%d" % b, "m3%d" % b], w=["m1%d" % b])
                    P.op("dve", lambda e, ys=ys, ya=ya, b=b: e.tensor_tensor(out=ob[b], in0=ys, in1=ya, op=ALU.add),
                         r=["m0%d" % b, "m1%d" % b], w=["ob%d" % b])
                    P.dma(mT[rows, sl], ob[b], r=["ob%d" % b])
            P.barrier()

        def phase_ffnmid(l):
            NFB = (2 * DFF) // 128
            fw = valloc(3 * NFB)
            src = ffn_conv_w[l].rearrange("k (r p) -> (k r) p", p=128)
            r0 = 0
            while r0 < 3 * NFB:
                rr = min(128, 3 * NFB - r0)
                vecT(src[r0:r0 + rr, :], rr, fw[:, r0:r0 + rr])
                r0 += rr
            fb = valloc(NFB)
            r0 = 0
            while r0 < NFB:
                rr = min(128, NFB - r0)
                vecT(vec_rows(ffn_conv_b[l], 2 * DFF)[r0:r0 + rr, :], rr, fb[:, r0:r0 + rr])
                r0 += rr
            ar = Arena()
            arr = [ar.F(516) for _ in range(2)]
            brr = [ar.F(516) for _ in range(2)]
            aa = [ar.F(512) for _ in range(2)]
            bb = [ar.F(512) for _ in range(2)]
            ob = [ar.B(512) for _ in range(2)]
            n = 0
            for tt in range(NT):
                t0 = tt * 512
                for j in range(DFF // 128):
                    b = n % 2
                    n += 1
                    for (buf, rbase, key) in ((arr[b], j * 128, "fa%d" % b), (brr[b], DFF + j * 128, "fb%d" % b)):
                        if tt == 0:
                            P.op("pool", lambda e, buf=buf: e.memset(buf[:, 0:2], 0.0), w=[key])
                            P.dma(buf[:, 2:514], upT[rbase:rbase + 128, 0:512], w=[key])
                        else:
                            P.dma(buf[:, 0:514], upT[rbase:rbase + 128, t0 - 2:t0 + 512], w=[key])
                    for (buf, acc_, blk, key, akey, eng) in ((arr[b], aa[b], j, "fa%d" % b, "aa%d" % b, "dve"),
                                                             (brr[b], bb[b], DFF // 128 + j, "fb%d" % b, "bb%d" % b,
                                                              "dve")):
                        for k in range(3):
                            wk = fw[:, k * NFB + blk:k * NFB + blk + 1]
                            if k == 0:
                                P.op(eng, lambda e, buf=buf, acc_=acc_, wk=wk: e.tensor_scalar(
                                    out=acc_, in0=buf[:, 0:512], scalar1=wk, scalar2=None, op0=ALU.mult),
                                    r=[key, "vecs"], w=[akey])
                            else:
                                P.op(eng, lambda e, buf=buf, acc_=acc_, wk=wk, k=k: e.scalar_tensor_tensor(
                                    out=acc_, in0=buf[:, k:k + 512], scalar=wk, in1=acc_, op0=ALU.mult, op1=ALU.add),
                                    r=[key, "vecs", akey], w=[akey])
                    P.op("act", lambda e, b=b, j=j: e.activation(out=aa[b], in_=aa[b], func=AF.Silu,
                                                                 bias=fb[:, j:j + 1]), r=["aa%d" % b, "vecs"],
                         w=["aa%d" % b])
                    jb = DFF // 128 + j
                    P.op("dve", lambda e, b=b, jb=jb: e.scalar_tensor_tensor(
                        out=ob[b], in0=bb[b], scalar=fb[:, jb:jb + 1], in1=aa[b], op0=ALU.add, op1=ALU.mult),
                        r=["bb%d" % b, "aa%d" % b, "vecs"], w=["ob%d" % b])
                    P.dma(fT[j * 128:(j + 1) * 128, t0:t0 + 512], ob[b], r=["ob%d" % b])
            P.barrier()

        def phase_final(resid, gate):
            ar = Arena()
            xt = [ar.F(D) for _ in range(2)]
            pt = ar.F(D)
            ot = [ar.F(D) for _ in range(2)]
            for i in range(NB):
                b = i % 2
                ts = slice(i * 128, (i + 1) * 128)
                x3 = xt[b].rearrange("p (k t) -> p k t", k=KC)
                p3 = pt.rearrange("p (k t) -> p k t", k=KC)
                P.dma(x3, xT[:, ts].rearrange("(k p) t -> p k t", p=128), w=["xt%d" % b])
                P.dma(p3, resid[:, ts].rearrange("(k p) t -> p k t", p=128), w=["pt"])
                for kc in range(KC):
                    P.op("dve", lambda e, kc=kc, x3=x3, p3=p3: e.scalar_tensor_tensor(
                        out=x3[:, kc, :], in0=p3[:, kc, :], scalar=gate[:, kc:kc + 1], in1=x3[:, kc, :], op0=ALU.mult,
                        op1=ALU.add), r=["xt%d" % b, "pt", "vecs"], w=["xt%d" % b])
                for k4 in range(0, KC, 4):
                    pi = (k4 // 4) % 2
                    for q in range(4):
                        P.op("pe", lambda e, pi=pi, q=q, k4=k4, x3=x3: e.transpose(
                            out=PB[pi][:, q * 128:(q + 1) * 128], in_=x3[:, k4 + q, :], identity=ident),
                            r=["xt%d" % b, "cst"], w=["pb%d" % pi])
                    P.op("act", lambda e, pi=pi, k4=k4, b=b: e.activation(
                        out=ot[b][:, k4 * 128:(k4 + 4) * 128], in_=PB[pi][:, :], func=AF.Copy),
                        r=["pb%d" % pi], w=["ot%d" % b])
                P.dma(out[ts, :], ot[b], r=["ot%d" % b])
            P.barrier()

        phase_tin()
        stop_after = os.environ.get("MK_STOP", "")
        resid, gate = None, None
        done = False
        for l in range(L):
            lv = LV[l]
            phase_norm(resid, gate, lv["gs1"], lv["shift1"])
            phase_gemm(hT, D, w_in[l], NIN, projT)
            if stop_after == "inproj":
                done = True
                break
            phase_ssdprep(l)
            phase_ssdmain(l)
            phase_ssdpost(l)
            phase_gemm(ynT, DI, w_ssd_out[l], D, yssdT)
            if stop_after == "ssd":
                done = True
                break
            phase_dsaprep(l)
            phase_dsaidx()
            phase_dsaatt()
            phase_gemm(oT, ATT, w_att_out[l], D, yattT)
            if stop_after == "dsa":
                done = True
                break
            phase_merge()
            phase_gemm(mT, D, w_o[l], D, pT)
            phase_norm(pT, lv["gate1"], lv["gs2"], lv["shift2"])
            phase_gemm(hT, D, w_up[l], 2 * DFF, upT)
            phase_ffnmid(l)
            phase_gemm(fT, DFF, w_down[l], D, dT)
            resid, gate = dT, lv["gate2"]
            if stop_after == "layer0":
                done = True
                break
        if not done:
            phase_final(resid, gate)
        P.finalize(es, block)
    return nc


def make_consts():
    c = np.zeros((128, 640), np.float32)
    c[:, 0:128] = np.eye(128, dtype=np.float32)
    c[:, 128:256] = 1.0
    j = np.arange(128)[:, None]
    i = np.arange(128)[None, :]
    c[:, 256:384] = (j <= i).astype(np.float32)
    c[:, 384:512] = np.where(j > i, -30000.0, 0.0)
    c[:, 512:640] = np.where(i > j, -1.0e30, 0.0)
    return c


WEIGHT_NAMES = ["w_ada", "b_ada", "ada_table", "norm1_g", "w_in", "ssd_conv_w", "ssd_conv_b", "ssd_dt_bias",
                "ssd_a_log", "ssd_d", "ssd_norm_g", "w_ssd_out", "q_norm_g", "k_norm_g", "idx_k_norm_g", "w_att_out",
                "w_o", "norm2_g", "w_up", "ffn_conv_w", "ffn_conv_b", "w_down"]


def run(cfg, inputs, debug=False, n_cores=8):
    nc = build_nc(cfg, debug=debug)
    B = inputs["x"].shape[0]
    consts = make_consts()
    shared = {k: np.ascontiguousarray(np.asarray(inputs[k], dtype=np.float32)) for k in WEIGHT_NAMES}
    shared["consts"] = consts
    in_maps = []
    for core in range(n_cores):
        b = (core * B) // n_cores
        m = dict(shared)
        m["x"] = np.ascontiguousarray(np.asarray(inputs["x"][b], dtype=np.float32))
        m["c"] = np.ascontiguousarray(np.asarray(inputs["c"][b], dtype=np.float32))
        in_maps.append(m)
    res = run_bass_kernel_spmd(nc, in_maps, core_ids=list(range(n_cores)))
    return res


def kernel(**inputs):
    cfg = Cfg(4096, 4096, 2)
    res = run(cfg, inputs)
    B = inputs["x"].shape[0]
    outs = [np.asarray(res.results[(b * 8) // B]["out"], dtype=np.float32) for b in range(B)]
    return np.stack(outs, axis=0)
```
